# Optimizing a Trainium2 kernel written in Bass

```python
import jax, jax.numpy as jnp
from jax import lax
import numpy as np

D_MODEL = 4096
BATCH = 2
SEQ = 8192
DEPTH = 2

HEAD_DIM = 128
BRANCH_WIDTH = D_MODEL // 2
RET_HEADS = D_MODEL // 512
RET_DK = HEAD_DIM
RET_DV = 2 * HEAD_DIM
RET_CHUNK = 128
RET_THETA = 10000.0
SWA_Q_HEADS = D_MODEL // 256
SWA_KV_HEADS = SWA_Q_HEADS // 4
SWA_WINDOW = 128
SWA_BLOCK = 128
ROPE_THETA = 500000.0
ROPE_DIM = HEAD_DIM // 4
LRU_WIDTH = D_MODEL // 2
LRU_BLOCKS = 16
LRU_BLOCK_DIM = LRU_WIDTH // LRU_BLOCKS
CONV_WIDTH = 4
LRU_C = 8.0
N_BRANCH = 3
D_FF = 4 * D_MODEL
EPS = 1e-6

IN_SIZES = (
    RET_HEADS * RET_DK,
    RET_HEADS * RET_DK,
    RET_HEADS * RET_DV,
    RET_HEADS * RET_DV,
    SWA_Q_HEADS * HEAD_DIM,
    SWA_KV_HEADS * HEAD_DIM,
    SWA_KV_HEADS * HEAD_DIM,
    LRU_WIDTH,
    LRU_WIDTH,
    N_BRANCH * D_MODEL,
)
IN_TOTAL = 25600

kernel_name = "hybrid_retention_swa_rglru_block"

F32 = jnp.float32


def _rms(x):
    xf = x.astype(F32)
    return xf * lax.rsqrt(jnp.mean(xf * xf, axis=-1, keepdims=True) + EPS)


def rmsnorm(x, gain):
    return (_rms(x) * gain.astype(F32)).astype(x.dtype)


def rope(x, positions, rot_dim, theta):
    half = rot_dim // 2
    freqs = theta ** (-jnp.arange(half, dtype=F32) / half)
    ang = positions.astype(F32)[:, :, None, None] * freqs
    cos, sin = jnp.cos(ang), jnp.sin(ang)
    xf = x.astype(F32)
    x1, x2, rest = xf[..., :half], xf[..., half:rot_dim], xf[..., rot_dim:]
    out = jnp.concatenate([x1 * cos - x2 * sin, x2 * cos + x1 * sin, rest], axis=-1)
    return out.astype(x.dtype)


def retention(q, k, v, g, positions):
    B, S, H, dk = q.shape
    dv = v.shape[-1]
    C = RET_CHUNK
    N = S // C
    qf = rope(q, positions, dk, RET_THETA).astype(F32)
    kf = rope(k, positions, dk, RET_THETA).astype(F32) * (dk ** -0.5)
    vf = v.astype(F32)
    log_g = jnp.log1p(-jnp.exp2(-5.0 - jnp.arange(H, dtype=F32)))
    idx = jnp.arange(C, dtype=F32)
    diff = idx[:, None] - idx[None, :]
    causal = diff >= 0
    decay_in = jnp.where(causal[None], jnp.exp(log_g[:, None, None] * jnp.where(causal, diff, 0.0)[None]), 0.0)
    xi = jnp.exp(log_g[:, None] * (idx + 1.0))
    zeta = jnp.exp(log_g[:, None] * (C - 1.0 - idx))
    g_chunk = jnp.exp(log_g * C)
    qc = qf.reshape(B, N, C, H, dk)
    kc = kf.reshape(B, N, C, H, dk)
    vc = vf.reshape(B, N, C, H, dv)
    scores = jnp.einsum('bnihd,bnjhd->bnhij', qc, kc) * decay_in
    intra = jnp.einsum('bnhij,bnjhv->bnihv', scores, vc)
    kv = jnp.einsum('bnjhd,bnjhv,hj->nbhdv', kc, vc, zeta)

    def step(state, kv_n):
        return g_chunk[None, :, None, None] * state + kv_n, state

    _, prev = lax.scan(step, jnp.zeros((B, H, dk, dv), F32), kv)
    cross = jnp.einsum('bnihd,hi,nbhdv->bnihv', qc, xi, prev)
    y = (intra + cross).reshape(B, S, H, dv)
    y = _rms(y)
    out = jax.nn.silu(g.astype(F32)) * y
    return out.reshape(B, S, H * dv).astype(q.dtype)


def swa_attention(q, k, v, q_gain, k_gain, sinks, positions):
    B, S, Hq, D = q.shape
    Hkv = k.shape[2]
    G = Hq // Hkv
    C = SWA_BLOCK
    N = S // C
    q = rope(rmsnorm(q, q_gain), positions, ROPE_DIM, ROPE_THETA)
    k = rope(rmsnorm(k, k_gain), positions, ROPE_DIM, ROPE_THETA)
    qb = q.reshape(B, N, C, Hkv, G, D)
    kb = k.reshape(B, N, C, Hkv, D)
    vb = v.reshape(B, N, C, Hkv, D)
    pad = jnp.zeros_like(kb[:, :1])
    k2 = jnp.concatenate([jnp.concatenate([pad, kb[:, :-1]], axis=1), kb], axis=2)
    v2 = jnp.concatenate([jnp.concatenate([pad, vb[:, :-1]], axis=1), vb], axis=2)
    s = jnp.einsum('bnqkgd,bnskd->bnkgqs', qb, k2).astype(F32) * (D ** -0.5)
    qi = jnp.arange(C)[:, None]
    sj = jnp.arange(2 * C)[None, :]
    rel = C + qi - sj
    band = (rel >= 0) & (rel < SWA_WINDOW)
    valid = band[None] & ((jnp.arange(N)[:, None, None] > 0) | (sj >= C)[None])
    s = jnp.where(valid[None, :, None, None], s, -jnp.inf)
    sink = sinks.astype(F32).reshape(Hkv, G)[None, None, :, :, None, None]
    m = jnp.maximum(jnp.max(s, axis=-1, keepdims=True), sink)
    p = jnp.exp(s - m)
    p = p / (jnp.sum(p, axis=-1, keepdims=True) + jnp.exp(sink - m))
    o = jnp.einsum('bnkgqs,bnskd->bnqkgd', p.astype(v.dtype), v2)
    return o.reshape(B, S, Hq * D)


def rglru_branch(xb, yb, conv_w, conv_b, wa, ba, wx, bx, lam):
    B, S, W = xb.shape
    xc = lax.conv_general_dilated(
        xb, conv_w[:, None, :].astype(xb.dtype), window_strides=(1,),
        padding=[(CONV_WIDTH - 1, 0)], dimension_numbers=('NWC', 'WIO', 'NWC'),
        feature_group_count=W) + conv_b.astype(xb.dtype)
    xh = xc.reshape(B, S, LRU_BLOCKS, LRU_BLOCK_DIM)
    r = jax.nn.sigmoid(jnp.einsum('bshi,hij->bshj', xh, wa).reshape(B, S, W).astype(F32) + ba.astype(F32))
    i = jax.nn.sigmoid(jnp.einsum('bshi,hij->bshj', xh, wx).reshape(B, S, W).astype(F32) + bx.astype(F32))
    log_a = -LRU_C * r * jax.nn.softplus(-lam.astype(F32))
    a = jnp.exp(log_a)
    u = jnp.sqrt(-jnp.expm1(2.0 * log_a)) * i * xc.astype(F32)

    def combine(left, right):
        a1, b1 = left
        a2, b2 = right
        return a1 * a2, a2 * b1 + b2

    _, h = lax.associative_scan(combine, (a, u), axis=1)
    return (h * jax.nn.gelu(yb.astype(F32))).astype(xb.dtype)


def hybrid_mixer(h, positions, w_in, swa_q_gain, swa_k_gain, swa_sinks, conv_w, conv_b,
                 lru_wa, lru_ba, lru_wx, lru_bx, lru_lambda, w_branch, w_out):
    B, S, _ = h.shape
    z = jnp.einsum('bsd,de->bse', h, w_in)
    split_points = np.cumsum(np.array(IN_SIZES))[:-1].tolist()
    rq, rk, rv, rg, sq, sk, sv, lx, ly, gates = jnp.split(z, split_points, axis=-1)
    o_ret = retention(rq.reshape(B, S, RET_HEADS, RET_DK), rk.reshape(B, S, RET_HEADS, RET_DK),
                      rv.reshape(B, S, RET_HEADS, RET_DV), rg.reshape(B, S, RET_HEADS, RET_DV), positions)
    o_swa = swa_attention(sq.reshape(B, S, SWA_Q_HEADS, HEAD_DIM), sk.reshape(B, S, SWA_KV_HEADS, HEAD_DIM),
                          sv.reshape(B, S, SWA_KV_HEADS, HEAD_DIM), swa_q_gain, swa_k_gain, swa_sinks, positions)
    o_lru = rglru_branch(lx, ly, conv_w, conv_b, lru_wa, lru_ba, lru_wx, lru_bx, lru_lambda)
    branches = jnp.stack([o_ret, o_swa, o_lru], axis=2)
    proj = jnp.einsum('bskw,kwd->bskd', branches, w_branch)
    gate = jax.nn.sigmoid(gates.reshape(B, S, N_BRANCH, D_MODEL))
    mixed = jnp.sum(gate * proj, axis=2)
    return jnp.einsum('bsd,de->bse', mixed, w_out)


def setup_inputs(seed: int = 0) -> dict:
    key = jax.random.key(seed)
    ks = jax.random.split(key, 20)
    L = DEPTH
    nrm = jax.random.normal
    x = nrm(ks[0], (BATCH, SEQ, D_MODEL), F32)
    offsets = jax.random.randint(ks[1], (BATCH, 1), 0, 4096, dtype=jnp.int32)
    positions = offsets + jnp.arange(SEQ, dtype=jnp.int32)[None, :]
    norm_mix = 1.0 + 0.01 * nrm(ks[2], (L, D_MODEL), F32)
    w_in = nrm(ks[3], (L, D_MODEL, IN_TOTAL), F32) * D_MODEL ** -0.5
    swa_q_gain = 1.0 + 0.01 * nrm(ks[4], (L, HEAD_DIM), F32)
    swa_k_gain = 1.0 + 0.01 * nrm(ks[5], (L, HEAD_DIM), F32)
    swa_sinks = 0.5 * nrm(ks[6], (L, SWA_Q_HEADS), F32)
    conv_w = nrm(ks[7], (L, CONV_WIDTH, LRU_WIDTH), F32) * CONV_WIDTH ** -0.5
    conv_b = 0.01 * nrm(ks[8], (L, LRU_WIDTH), F32)
    lru_wa = nrm(ks[9], (L, LRU_BLOCKS, LRU_BLOCK_DIM, LRU_BLOCK_DIM), F32) * LRU_BLOCK_DIM ** -0.5
    lru_ba = 0.01 * nrm(ks[10], (L, LRU_WIDTH), F32)
    lru_wx = nrm(ks[11], (L, LRU_BLOCKS, LRU_BLOCK_DIM, LRU_BLOCK_DIM), F32) * LRU_BLOCK_DIM ** -0.5
    lru_bx = 0.01 * nrm(ks[12], (L, LRU_WIDTH), F32)
    a_c = jax.random.uniform(ks[13], (L, LRU_WIDTH), F32, 0.9, 0.999)
    a0 = a_c ** (1.0 / LRU_C)
    lru_lambda = jnp.log(a0) - jnp.log1p(-a0)
    w_branch = nrm(ks[14], (L, N_BRANCH, BRANCH_WIDTH, D_MODEL), F32) * BRANCH_WIDTH ** -0.5
    w_out = nrm(ks[15], (L, D_MODEL, D_MODEL), F32) * D_MODEL ** -0.5
    norm_mlp = 1.0 + 0.01 * nrm(ks[16], (L, D_MODEL), F32)
    w_mlp_in = nrm(ks[17], (L, D_MODEL, D_FF), F32) * D_MODEL ** -0.5
    w_mlp_out = nrm(ks[18], (L, D_FF, D_MODEL), F32) * D_FF ** -0.5
    return {"x": x, "positions": positions, "norm_mix": norm_mix, "w_in": w_in,
            "swa_q_gain": swa_q_gain, "swa_k_gain": swa_k_gain, "swa_sinks": swa_sinks,
            "conv_w": conv_w, "conv_b": conv_b, "lru_wa": lru_wa, "lru_ba": lru_ba,
            "lru_wx": lru_wx, "lru_bx": lru_bx, "lru_lambda": lru_lambda,
            "w_branch": w_branch, "w_out": w_out, "norm_mlp": norm_mlp,
            "w_mlp_in": w_mlp_in, "w_mlp_out": w_mlp_out}


def reference(x, positions, norm_mix, w_in, swa_q_gain, swa_k_gain, swa_sinks, conv_w, conv_b,
              lru_wa, lru_ba, lru_wx, lru_bx, lru_lambda, w_branch, w_out, norm_mlp,
              w_mlp_in, w_mlp_out):
    for l in range(DEPTH):
        h = rmsnorm(x, norm_mix[l])
        x = x + hybrid_mixer(h, positions, w_in[l], swa_q_gain[l], swa_k_gain[l], swa_sinks[l],
                             conv_w[l], conv_b[l], lru_wa[l], lru_ba[l], lru_wx[l], lru_bx[l],
                             lru_lambda[l], w_branch[l], w_out[l])
        h = rmsnorm(x, norm_mlp[l])
        u = jnp.einsum('bsd,df->bsf', h, w_mlp_in[l])
        x = x + jnp.einsum('bsf,fd->bsd', jnp.square(jax.nn.relu(u)), w_mlp_out[l])
    return x
```

```python
import math
import os
from contextlib import ExitStack
import numpy as np
import ml_dtypes
import concourse.bass as bass
import concourse.mybir as mybir
from concourse.bass_utils import run_bass_kernel_spmd

F32, BF16, I32 = mybir.dt.float32, mybir.dt.bfloat16, mybir.dt.int32
ALU = mybir.AluOpType
AF = mybir.ActivationFunctionType
AX = mybir.AxisListType
import os
STOP = int(os.environ.get('KSTOP', '8'))
KN = int(os.environ.get('KN', '9'))
SAME = os.environ.get('KSAME', '1') == '1'
EPS = 1e-6
PI = math.pi


class Cfg:
    def __init__(self, D=4096, S=8192, L=2, T=512):
        self.D, self.S, self.L, self.T = D, S, L, T
        self.KT = D // 128
        self.RH = D // 512
        self.QH = D // 256
        self.KVH = self.QH // 4
        self.W = D // 2
        self.LB = self.W // 128
        self.F = 4 * D
        self.FT = self.F // 128
        self.NB = T // 128
        self.NT = S // T
        RH, QH, KVH, W = self.RH, self.QH, self.KVH, self.W
        o = 0
        self.o_rq = o; o += RH * 128
        self.o_rk = o; o += RH * 128
        self.o_rv = o; o += RH * 256
        self.o_rg = o; o += RH * 256
        self.o_sq = o; o += QH * 128
        self.o_sk = o; o += KVH * 128
        self.o_sv = o; o += KVH * 128
        self.o_lx = o; o += W
        self.o_ly = o; o += W
        self.o_gt = o; o += 3 * D
        self.IN = o


class Src:
    def __init__(self, sem, unit):
        self.sem, self.unit, self.n = sem, unit, 0


class Eng(Src):
    def __init__(self, sem, name):
        super().__init__(sem, 1)
        self.name = name
        self.prog = []
        self.seen = {}

    def wait(self, toks):
        for t in toks:
            if t is None:
                continue
            s, n = t
            if s is self and not SAME:
                continue
            if self.seen.get(id(s), 0) >= n:
                continue
            self.seen[id(s)] = n
            self.prog.append((0, s.sem, n * s.unit))

    def replay(self, e):
        for it in self.prog:
            if it[0] == 0:
                e.wait_ge(it[1], it[2])
            else:
                ins = it[1](e)
                if it[2] is not None:
                    ins.then_inc(it[2], it[3])


class Buf:
    def __init__(self):
        self.w = {}
        self.r = {}

    def rdeps(self):
        return list(self.w.values())

    def wdeps(self):
        return list(self.w.values()) + list(self.r.values())

    def read(self, tok):
        k = id(tok[0])
        if k not in self.r or self.r[k][1] < tok[1]:
            self.r[k] = tok

    def wrote(self, tok):
        k = id(tok[0])
        if k not in self.w or self.w[k][1] < tok[1]:
            self.w[k] = tok


def _deps(R, Wr, extra):
    d = []
    for b in R:
        d += b.rdeps()
    for b in Wr:
        d += b.wdeps()
    d += list(extra)
    return d


def op(eng, fn, R=(), Wr=(), extra=()):
    eng.wait(_deps(R, Wr, extra))
    eng.n += 1
    eng.prog.append((1, fn, eng.sem, 1))
    tok = (eng, eng.n)
    for b in R:
        b.read(tok)
    for b in Wr:
        b.wrote(tok)
    return tok


def group(eng, fns, R=(), Wr=(), extra=()):
    eng.wait(_deps(R, Wr, extra))
    for f in fns[:-1]:
        eng.prog.append((1, f, None, 0))
    eng.n += 1
    eng.prog.append((1, fns[-1], eng.sem, 1))
    tok = (eng, eng.n)
    for b in R:
        b.read(tok)
    for b in Wr:
        b.wrote(tok)
    return tok


def dma(q, ds, fns, R=(), Wr=(), extra=()):
    q.wait(_deps(R, Wr, extra))
    for f in fns:
        ds.n += 1
        q.prog.append((1, f, ds.sem, 16))
    tok = (ds, ds.n)
    for b in R:
        b.read(tok)
    for b in Wr:
        b.wrote(tok)
    return tok


def build(cfg):
    D, S, L, T, KT, RH, QH, KVH, W, LB, F, FT, NB, NT, IN = (
        cfg.D, cfg.S, cfg.L, cfg.T, cfg.KT, cfg.RH, cfg.QH, cfg.KVH, cfg.W, cfg.LB, cfg.F,
        cfg.FT, cfg.NB, cfg.NT, cfg.IN)
    NBLK = S // 128
    nc = bass.Bass("TRN2", target_bir_lowering=False)

    def din(name, shape, dt=F32):
        return nc.dram_tensor(name, list(shape), dt, kind="ExternalInput")

    x_d = din("x", [S, D]); pos_d = din("pos", [128, NBLK], I32)
    win_d = din("w_in", [L, D, IN]); wbr_d = din("w_branch", [L, 3 * W, D])
    wout_d = din("w_out", [L, D, D]); wmi_d = din("w_mlp_in", [L, D, F]); wmo_d = din("w_mlp_out", [L, F, D])
    g1_d = din("g1", [128, L * KT]); g2_d = din("g2", [128, L * KT])
    qg_d = din("qg", [128, L * 128]); kg_d = din("kg", [128, L * 128]); sk_d = din("sinkb", [128, L * QH])
    cw_d = din("cw", [128, L * LB * 4]); cb_d = din("cb", [128, L * LB]); ba_d = din("ba", [128, L * LB])
    bx_d = din("bx", [128, L * LB]); lam_d = din("lam", [128, L * LB])
    wa_d = din("wa", [L, 128, LB * 128]); wx_d = din("wx", [L, 128, LB * 128])
    id_d = din("ident", [128, 128]); dt_d = din("dtab", [128, RH * 128], BF16)
    xi_d = din("xi", [128, RH]); ze_d = din("zeta", [128, RH]); gc_d = din("gch", [128, RH])
    mc_d = din("mcur", [128, 512], BF16); mp_d = din("mprev", [128, 512], BF16)
    fr_d = din("fr_ret", [128, 128]); fs_d = din("fr_swa", [128, 64])
    y_d = nc.dram_tensor("y", [S, D], F32, kind="ExternalOutput")
    xs_d = [y_d for i in range(max(L - 1, 0))]
    rst_d = nc.dram_tensor("rstate", [128, RH * 256], F32)

    es = ExitStack()
    sb = lambda name, shape, dt=F32: es.enter_context(nc.sbuf_tensor(name, list(shape), dt))
    psm = lambda name, shape, dt=F32: es.enter_context(nc.psum_tensor(name, list(shape), dt))
    sem = lambda name: es.enter_context(nc.semaphore(name))

    with es:
        PE = Eng(sem("s_pe"), "pe"); ACT = Eng(sem("s_act"), "act"); DVE = Eng(sem("s_dve"), "dve")
        PL = Eng(sem("s_pl"), "pool"); SP = Eng(sem("s_sp"), "sp")
        ring_ds = [Src(sem("s_r0"), 16), Src(sem("s_r1"), 16)]
        xl_ds = [Src(sem("s_xl%d" % b), 16) for b in range(NB)]
        xs_ds = Src(sem("s_xs"), 16); c_ds = Src(sem("s_c"), 16)
        top_ds = Src(sem("s_top"), 16); stl_ds = Src(sem("s_stl"), 16); sts_ds = Src(sem("s_sts"), 16)

        OTN = 3 * LB * T
        RAN = max(NB * D, OTN // 2 + 4096)
        RCN = max(KT * T, 4 * D, 16384)
        RA = sb("RA", [128, RAN]); RAb = RA.bitcast(BF16)
        RB = sb("RB", [128, KT * T], BF16)
        RC = sb("RC", [128, RCN], BF16); RCf = RC.bitcast(F32); RCi = RC.bitcast(I32)
        TOPB = 2 * NB * 128 * 4 + 2 * NB * 64 * 4 + RH * 128 * 2 + 2 * 1024
        top0 = RCN * 2 - TOPB
        _o = top0
        cosr = RCf[:, _o // 4:_o // 4 + NB * 128]; _o += NB * 128 * 4
        sinr = RCf[:, _o // 4:_o // 4 + NB * 128]; _o += NB * 128 * 4
        coss = RCf[:, _o // 4:_o // 4 + NB * 64]; _o += NB * 64 * 4
        sins = RCf[:, _o // 4:_o // 4 + NB * 64]; _o += NB * 64 * 4
        dtb = RC[:, _o // 2:_o // 2 + RH * 128]; _o += RH * 128 * 2
        mcur = RC[:, _o // 2:_o // 2 + 512]; _o += 1024
        mprev = RC[:, _o // 2:_o // 2 + 512]; _o += 1024
        ring = [sb("ring%d" % i, [128, 32, 512], BF16) for i in range(2)]
        bring = [Buf(), Buf()]
        bRA, bhT, bRC = Buf(), Buf(), Buf()
        bx = [bRA] * NB
        identb = sb("identb", [128, 128], BF16)
        xi = sb("xi_s", [128, RH]); zeta = sb("zeta_s", [128, RH]); gch = sb("gch_s", [128, RH])
        frr = sb("frr", [128, 128]); frs = sb("frs", [128, 64])
        posi = sb("posi", [128, NBLK], I32); posf = sb("posf", [128, NBLK])
        g1 = sb("g1s", [128, L * KT]); g2 = sb("g2s", [128, L * KT])
        qg = sb("qgs", [128, L * 128]); kg = sb("kgs", [128, L * 128]); esk = sb("esk", [128, L * QH])
        cw = sb("cws", [128, L * LB * 4]); cb = sb("cbs", [128, L * LB]); bas = sb("bas", [128, L * LB])
        bxs = sb("bxs", [128, L * LB]); lam = sb("lams", [128, L * LB]); cc = sb("ccs", [128, L * LB]); cc2 = sb("cc2s", [128, L * LB])
        bconst = Buf()
        if os.environ.get('KPAD'):
            _pad = sb('pad', [128, int(os.environ['KPAD']) // 4])
        print('SBUF remaining after allocs so far:', nc.sbuf_bytes_remaining)
        stv = sb("stv", [128, 256]); stb = sb("stb", [128, 256], BF16); bdr = [Buf() for _ in range(RH)]
        kTh = sb("kTh", [128, KVH * 128], BF16); vh = sb("vh", [128, KVH * 128], BF16)
        ones = sb("ones", [128, 1], BF16)
        lhalo = sb("lhalo", [128, LB * 3]); lhs = sb("lhs", [128, LB])
        brst, bstb, bkTh, bvh, blh = Buf(), Buf(), Buf(), Buf(), Buf()
        btab = Buf()
        small = sb("small", [128, 64]); bsm = Buf()
        rtmp = sb("rtmp", [128, 512]); brt = [Buf(), Buf()]
        acc = [psm("acc%d" % i, [128, 512]) for i in range(3)]; bacc = [Buf() for _ in range(3)]
        scp = [psm("sc%d" % i, [128, 512]) for i in range(2)]; bsc = [Buf(), Buf()]
        yo = psm("yo", [128, 512]); byo = Buf()
        tp = psm("tp", [128, 2048], BF16); btp = [Buf(), Buf()]
        st = {"acc": 0, "tp": 0, "ck": 0}

        DBG = os.environ.get('KDBG')
        dbg_done = set()

        def dbg(name, ap, n, dt, bufs):
            if DBG != name or name in dbg_done:
                return
            dbg_done.add(name)
            d = nc.dram_tensor("dbg", [128, n], dt, kind="ExternalOutput")
            dma(SP, xs_ds, [lambda e: e.dma_start(out=d[:, :], in_=ap)], R=bufs)

        def fence(buf=None):
            buf = buf if buf is not None else Buf()
            for E_ in (PE, ACT, DVE):
                if E_.n > 0:
                    buf.wrote((E_, E_.n))
            return buf

        def nacc():
            i = st["acc"] % 3; st["acc"] += 1
            return acc[i], bacc[i]

        def ntp():
            j = st["tp"] % 2; st["tp"] += 1
            return j * 1024, btp[j]

        class Scr:
            def __init__(self):
                self.off = 0
            def get(self, n, dt):
                b = 2 if dt == BF16 else 4
                o = (self.off + 31) // 32 * 32
                self.off = o + n * b
                assert self.off <= top0, "RC scratch overflow"
                if dt == I32:
                    return RCi[:, o // 4:o // 4 + n], fence()
                if dt == F32:
                    return RCf[:, o // 4:o // 4 + n], fence()
                return RC[:, o // 2:o // 2 + n], fence()

        chunks = []

        def seg_cols(wd, l, c0, w, dcol=0, kt=None, r0=0):
            kt = kt if kt is not None else KT
            src = wd[l, r0:r0 + kt * 128, c0:c0 + w].rearrange("(k p) c -> p k c", p=128)
            out = []
            step = kt
            for k0 in range(0, kt, step):
                k1 = min(kt, k0 + step)
                out.append((lambda slot, k0=k0, k1=k1: slot[:, k0:k1, dcol:dcol + w], src[:, k0:k1, :]))
            return out

        def plan_tile(l):
            cl = []
            for hh in range(RH if STOP >= 2 else 0):
                cl.append(seg_cols(win_d, l, cfg.o_rq + hh * 128, 128, 0) + seg_cols(win_d, l, cfg.o_rk + hh * 128, 128, 128))
                cl.append(seg_cols(win_d, l, cfg.o_rv + hh * 256, 256, 0) + seg_cols(win_d, l, cfg.o_rg + hh * 256, 256, 256))
            for kh in range(KVH if STOP >= 3 else 0):
                cl.append(seg_cols(win_d, l, cfg.o_sk + kh * 128, 128, 0) + seg_cols(win_d, l, cfg.o_sv + kh * 128, 128, 128))
                cl.append(seg_cols(win_d, l, cfg.o_sq + kh * 512, 512, 0))
            for lb in range(LB if STOP >= 4 else 0):
                c = seg_cols(win_d, l, cfg.o_lx + lb * 128, 128, 0) + seg_cols(win_d, l, cfg.o_ly + lb * 128, 128, 128)
                c.append((lambda slot: slot[:, 0, 256:384], wa_d[l, :, lb * 128:(lb + 1) * 128]))
                c.append((lambda slot: slot[:, 1, 256:384], wx_d[l, :, lb * 128:(lb + 1) * 128]))
                cl.append(c)
            for dg in range(D // 512 if STOP >= 5 else 0):
                for k in range(3):
                    cl.append(seg_cols(win_d, l, cfg.o_gt + k * D + dg * 512, 512, 0))
                    cl.append(seg_cols(wbr_d, l, dg * 512, 512, 0, kt=LB, r0=k * W))
            for dg in range(D // 512 if STOP >= 6 else 0):
                cl.append(seg_cols(wout_d, l, dg * 512, 512, 0))
            for fc in range(FT // 16 if STOP >= 8 else 0):
                for q in range(4):
                    cl.append(seg_cols(wmi_d, l, fc * 2048 + q * 512, 512, 0))
                for dg in range(D // 512):
                    cl.append(seg_cols(wmo_d, l, dg * 512, 512, 0, kt=16, r0=fc * 2048))
            return cl

        for l in range(L):
            pl = plan_tile(l)
            for t in range(NT):
                chunks.extend(pl)
        issued = [0]

        def issue_fill(i):
            s = i % 2
            segs = chunks[i]
            fns = [(lambda e, d=d, sr=sr, s=s: e.dma_start(out=d(ring[s]), in_=sr)) for (d, sr) in segs]
            dma(PL, ring_ds[s], fns, Wr=[bring[s]])

        def next_chunk():
            i = st["ck"]; st["ck"] += 1
            while issued[0] <= min(i + 1, len(chunks) - 1) and len(chunks) > 0:
                issue_fill(issued[0]); issued[0] += 1
            return ring[i % 2], bring[i % 2]

        def cload(dst, src, eng=None):
            dma(PL, c_ds, [lambda e: e.dma_start(out=dst, in_=src)], Wr=[bconst])

        cload(identb[:, :], id_d[:, :])
        cload(xi[:, :], xi_d[:, :]); cload(zeta[:, :], ze_d[:, :]); cload(gch[:, :], gc_d[:, :])
        cload(frr[:, :], fr_d[:, :]); cload(frs[:, :], fs_d[:, :]); cload(posi[:, :], pos_d[:, :])
        cload(g1[:, :], g1_d[:, :]); cload(g2[:, :], g2_d[:, :]); cload(qg[:, :], qg_d[:, :]); cload(kg[:, :], kg_d[:, :])
        cload(esk[:, :], sk_d[:, :]); cload(cw[:, :], cw_d[:, :]); cload(cb[:, :], cb_d[:, :]); cload(bas[:, :], ba_d[:, :])
        cload(bxs[:, :], bx_d[:, :]); cload(lam[:, :], lam_d[:, :])
        op(DVE, lambda e: e.tensor_copy(out=posf[:, :], in_=posi[:, :]), R=[bconst], Wr=[bconst])
        op(DVE, lambda e: e.memset(ones[:, :], 1.0), Wr=[bconst])
        op(ACT, lambda e: e.activation(out=esk[:, :], in_=esk[:, :], func=AF.Exp), R=[bconst], Wr=[bconst])
        op(DVE, lambda e: e.tensor_scalar(out=qg[:, :], in0=qg[:, :], scalar1=128.0 ** -0.5, scalar2=None, op0=ALU.mult), R=[bconst], Wr=[bconst])
        op(ACT, lambda e: e.activation(out=cc[:, :], in_=lam[:, :], func=AF.Exp, scale=-1.0), R=[bconst], Wr=[bconst])
        op(ACT, lambda e: e.activation(out=cc[:, :], in_=cc[:, :], func=AF.Ln, bias=1.0), R=[bconst], Wr=[bconst])
        op(DVE, lambda e: e.tensor_scalar(out=cc2[:, :], in0=cc[:, :], scalar1=-16.0, scalar2=None, op0=ALU.mult), R=[bconst], Wr=[bconst])
        op(DVE, lambda e: e.tensor_scalar(out=cc[:, :], in0=cc[:, :], scalar1=-8.0, scalar2=None, op0=ALU.mult), R=[bconst], Wr=[bconst])

        def rstd(dst, src, n, bdst):
            op(ACT, lambda e: e.activation(out=dst, in_=src, func=AF.Sqrt, scale=1.0 / n, bias=EPS), R=[bdst], Wr=[bdst])
            op(DVE, lambda e: e.reciprocal(out=dst, in_=dst), R=[bdst], Wr=[bdst])

        def load_x(xin, t):
            for b in range(NB):
                r0 = t * T + b * 128
                dma(SP, xl_ds[b], [lambda e, b=b, r0=r0: e.dma_start(out=RA[:, b * D:(b + 1) * D], in_=xin[r0:r0 + 128, :])], Wr=[bRA])

        def tables(t):
            fence(btab)
            dma(SP, top_ds, [lambda e: e.dma_start(out=dtb, in_=dt_d[:, :]), lambda e: e.dma_start(out=mcur, in_=mc_d[:, :]),
                             lambda e: e.dma_start(out=mprev, in_=mp_d[:, :])], Wr=[btab, bconst])
            sc_ = Scr()
            angt, bang = sc_.get(128, F32); angf, _ = sc_.get(128, F32); angi, _ = sc_.get(128, I32)
            for b in range(NB):
                blk = t * NB + b
                for (fr, n, ct, sn) in ((frr, 128, cosr, sinr), (frs, 64, coss, sins)):
                    for (shift, dst) in ((0.0, sn), (0.25, ct)):
                        op(DVE, lambda e, fr=fr, n=n, blk=blk, shift=shift: e.tensor_scalar(out=angt[:, 0:n], in0=fr[:, 0:n], scalar1=posf[:, blk:blk + 1], scalar2=shift, op0=ALU.mult, op1=ALU.add), R=[bconst], Wr=[bang])
                        op(DVE, lambda e, n=n: e.tensor_copy(out=angi[:, 0:n], in_=angt[:, 0:n]), R=[bang], Wr=[bang])
                        op(DVE, lambda e, n=n: e.tensor_copy(out=angf[:, 0:n], in_=angi[:, 0:n]), R=[bang], Wr=[bang])
                        op(DVE, lambda e, n=n: e.tensor_tensor(out=angt[:, 0:n], in0=angt[:, 0:n], in1=angf[:, 0:n], op=ALU.subtract), R=[bang], Wr=[bang])
                        op(DVE, lambda e, n=n: e.scalar_tensor_tensor(out=angf[:, 0:n], in0=angt[:, 0:n], scalar=0.5, in1=angt[:, 0:n], op0=ALU.is_gt, op1=ALU.subtract), R=[bang], Wr=[bang])
                        op(ACT, lambda e, n=n, dst=dst, b=b: e.activation(out=dst[:, b * n:(b + 1) * n], in_=angf[:, 0:n], func=AF.Sin, scale=-2 * PI * (1 - 1e-6)), R=[bang], Wr=[btab])

        def norm_phase(gain, goff):
            fence(bRC)
            bxn = [fence(), fence()]; bjunk = fence()
            for b in range(NB):
                xb = RA[:, b * D:(b + 1) * D]
                xn = RC[:, 0:D]
                op(ACT, lambda e, xb=xb: e.activation(out=RC[:, D:2 * D], in_=xb, func=AF.Square, accum_out=small[:, 0:1]), R=[bRA], Wr=[bjunk, bsm, bRC])
                rstd(small[:, 0:1], small[:, 0:1], D, bsm)
                if KN < 2: continue
                op(ACT, lambda e, xb=xb, xn=xn: e.activation(out=xn, in_=xb, func=AF.Copy, scale=small[:, 0:1]), R=[bRA, bsm], Wr=[bxn[b % 2], bRC])
                if KN < 3: continue
                for k0 in range(0, KT, 4):
                    kk = min(4, KT - k0)
                    o, bt = ntp()
                    group(PE, [(lambda e, i=i, o=o, xn=xn, k0=k0: e.transpose(out=tp[:, o + i * 128:o + (i + 1) * 128], in_=xn[:, (k0 + i) * 128:(k0 + i + 1) * 128], identity=identb[:, :])) for i in range(kk)],
                          R=[bxn[b % 2], bconst], Wr=[bt])
                    if KN < 4: continue
                    for i in range(kk):
                        k = k0 + i
                        dst = RB[:, k * T + b * 128:k * T + (b + 1) * 128]
                        if k % 2 == 0:
                            op(DVE, lambda e, i=i, o=o, dst=dst: e.tensor_copy(out=dst, in_=tp[:, o + i * 128:o + (i + 1) * 128]), R=[bt], Wr=[bhT])
                        elif KN >= 5:
                            op(ACT, lambda e, i=i, o=o, dst=dst: e.activation(out=dst, in_=tp[:, o + i * 128:o + (i + 1) * 128], func=AF.Copy), R=[bt], Wr=[bhT])
                        if KN >= 6:
                            op(DVE, lambda e, dst=dst, k=k: e.tensor_scalar(out=dst, in0=dst, scalar1=gain[:, goff + k:goff + k + 1], scalar2=None, op0=ALU.mult), R=[bhT, bconst], Wr=[bhT])

        def oT(kt, c0, c1):
            return RAb[:, kt * T + c0:kt * T + c1]

        def proj_tok(b, slot, bslot, c0, w, pa, ba_):
            group(PE, [(lambda e, k=k: e.matmul(pa[:, 0:w], lhsT=RB[:, k * T + b * 128:k * T + (b + 1) * 128], rhs=slot[:, k, c0:c0 + w], start=(k == 0), stop=(k == KT - 1))) for k in range(KT)],
                  R=[bhT, bslot], Wr=[ba_])

        def proj_feat(slot, bslot, c0, pa, ba_, rhs_fn, nk, rbufs):
            group(PE, [(lambda e, k=k: e.matmul(pa[:, 0:T], lhsT=slot[:, k, c0:c0 + 128], rhs=rhs_fn(k), start=(k == 0), stop=(k == nk - 1))) for k in range(nk)],
                  R=rbufs + [bslot], Wr=[ba_])

        def retention(l, t):
            sc_ = Scr()
            t1, bt1 = sc_.get(128, F32); t2, bt2 = sc_.get(128, F32); t3, bt3 = sc_.get(128, F32); t4, bt4 = sc_.get(128, F32)
            sg, bsg = sc_.get(256, F32); junk, bjk = sc_.get(256, F32)
            qkrA, _ = sc_.get(NB * 256, BF16); qkTA, _ = sc_.get(NB * 384, BF16); qx, bqx = sc_.get(128, BF16)
            bqkr = [fence() for _ in range(NB)]; bqkT = [fence() for _ in range(NB)]
            vbf, bvbf = sc_.get(256, BF16); vz, bvz = sc_.get(256, BF16); STt, bST = sc_.get(128, BF16); obf, bobf = sc_.get(256, BF16)
            v3 = lambda ap: ap.rearrange("p (w c) -> p w c", w=2)
            for hh in range(RH):
                stv_ = stv[:, :]
                if t == 0:
                    op(DVE, lambda e: e.memset(stv[:, :], 0.0), Wr=[brst])
                else:
                    dma(SP, stl_ds, [lambda e, hh=hh: e.dma_start(out=stv[:, :], in_=rst_d[:, hh * 256:(hh + 1) * 256])], R=[bdr[hh]], Wr=[brst])
                op(ACT, lambda e: e.activation(out=stb[:, :], in_=stv[:, :], func=AF.Copy), R=[brst], Wr=[bstb])
                s1, bs1 = next_chunk()
                for b in range(NB):
                    qkr = qkrA[:, b * 256:(b + 1) * 256]; qkT = qkTA[:, b * 384:(b + 1) * 384]
                    pq, bpq = nacc(); proj_tok(b, s1, bs1, 0, 256, pq, bpq)
                    x4 = pq[:, 0:256].rearrange("p (w h c) -> p w h c", w=2, h=2)
                    x1, x2 = x4[:, :, 0, :], x4[:, :, 1, :]
                    cs = v3(cosr[:, b * 128:(b + 1) * 128]); sn = v3(sinr[:, b * 128:(b + 1) * 128])
                    op(DVE, lambda e, x1=x1, cs=cs: e.tensor_tensor(out=v3(t1), in0=x1, in1=cs, op=ALU.mult), R=[bpq, btab], Wr=[bt1])
                    op(DVE, lambda e, x2=x2, sn=sn: e.tensor_tensor(out=v3(t2), in0=x2, in1=sn, op=ALU.mult), R=[bpq, btab], Wr=[bt2])
                    op(DVE, lambda e, x2=x2, cs=cs: e.tensor_tensor(out=v3(t3), in0=x2, in1=cs, op=ALU.mult), R=[bpq, btab], Wr=[bt3])
                    op(DVE, lambda e, x1=x1, sn=sn: e.tensor_tensor(out=v3(t4), in0=x1, in1=sn, op=ALU.mult), R=[bpq, btab], Wr=[bt4])
                    q4 = qkr.rearrange("p (w h c) -> p w h c", w=2, h=2)
                    op(DVE, lambda e, q4=q4: e.tensor_tensor(out=q4[:, :, 0, :], in0=v3(t1), in1=v3(t2), op=ALU.subtract), R=[bt1, bt2], Wr=[bqkr[b]])
                    op(DVE, lambda e, q4=q4: e.tensor_tensor(out=q4[:, :, 1, :], in0=v3(t3), in1=v3(t4), op=ALU.add), R=[bt3, bt4], Wr=[bqkr[b]])
                    op(ACT, lambda e, hh=hh, qkr=qkr: e.activation(out=qx, in_=qkr[:, 0:128], func=AF.Copy, scale=xi[:, hh:hh + 1]), R=[bqkr[b], bconst], Wr=[bqx])
                    o, bt = ntp()
                    group(PE, [lambda e, o=o, qkr=qkr: e.transpose(out=tp[:, o:o + 128], in_=qkr[:, 0:128], identity=identb[:, :]),
                               lambda e, o=o, qkr=qkr: e.transpose(out=tp[:, o + 128:o + 256], in_=qkr[:, 128:256], identity=identb[:, :]),
                               lambda e, o=o: e.transpose(out=tp[:, o + 256:o + 384], in_=qx, identity=identb[:, :])], R=[bqkr[b], bqx, bconst], Wr=[bt])
                    op(ACT, lambda e, o=o, qkT=qkT: e.activation(out=qkT, in_=tp[:, o:o + 384], func=AF.Copy), R=[bt], Wr=[bqkT[b]])
                s2, bs2 = next_chunk()
                for b in range(NB):
                    qkr = qkrA[:, b * 256:(b + 1) * 256]; qkT = qkTA[:, b * 384:(b + 1) * 384]
                    pv, bpv = nacc(); proj_tok(b, s2, bs2, 0, 512, pv, bpv)
                    group(PE, [lambda e, qkT=qkT: e.matmul(scp[0][:, 0:128], lhsT=qkT[:, 128:256], rhs=qkT[:, 0:128], start=True, stop=True)], R=[bqkT[b]], Wr=[bsc[0]])
                    op(DVE, lambda e, hh=hh: e.tensor_tensor(out=STt, in0=scp[0][:, 0:128], in1=dtb[:, hh * 128:(hh + 1) * 128], op=ALU.mult), R=[bsc[0], bconst], Wr=[bST])
                    op(ACT, lambda e, pv=pv: e.activation(out=vbf, in_=pv[:, 0:256], func=AF.Copy), R=[bpv], Wr=[bvbf])
                    op(DVE, lambda e, pv=pv, hh=hh: e.tensor_scalar(out=vz, in0=pv[:, 0:256], scalar1=zeta[:, hh:hh + 1], scalar2=None, op0=ALU.mult), R=[bpv, bconst], Wr=[bvz])
                    op(ACT, lambda e, pv=pv: e.activation(out=sg, in_=pv[:, 256:512], func=AF.Silu), R=[bpv], Wr=[bsg])
                    group(PE, [lambda e: e.matmul(yo[:, 0:256], lhsT=STt, rhs=vbf, start=True, stop=False),
                               lambda e, qkT=qkT: e.matmul(yo[:, 0:256], lhsT=qkT[:, 256:384], rhs=stb[:, :], start=False, stop=True)], R=[bST, bvbf, bqkT[b], bstb], Wr=[byo])
                    pk, bpk = nacc()
                    group(PE, [lambda e, pk=pk, qkr=qkr: e.matmul(pk[:, 0:256], lhsT=qkr[:, 128:256], rhs=vz, start=True, stop=True)], R=[bqkr[b], bvz], Wr=[bpk])
                    op(DVE, lambda e, pk=pk, hh=hh: e.scalar_tensor_tensor(out=stv[:, :], in0=stv[:, :], scalar=gch[:, hh:hh + 1], in1=pk[:, 0:256], op0=ALU.mult, op1=ALU.add), R=[bpk, bconst, brst], Wr=[brst])
                    op(ACT, lambda e: e.activation(out=stb[:, :], in_=stv[:, :], func=AF.Copy), R=[brst], Wr=[bstb])
                    op(ACT, lambda e: e.activation(out=junk, in_=yo[:, 0:256], func=AF.Square, accum_out=small[:, 1:2]), R=[byo], Wr=[bjk, bsm])
                    rstd(small[:, 1:2], small[:, 1:2], 256, bsm)
                    op(DVE, lambda e: e.scalar_tensor_tensor(out=obf, in0=yo[:, 0:256], scalar=small[:, 1:2], in1=sg, op0=ALU.mult, op1=ALU.mult), R=[byo, bsm, bsg], Wr=[bobf])
                    o2, bt2_ = ntp()
                    group(PE, [lambda e, o2=o2: e.transpose(out=tp[:, o2:o2 + 128], in_=obf[:, 0:128], identity=identb[:, :]),
                               lambda e, o2=o2: e.transpose(out=tp[:, o2 + 128:o2 + 256], in_=obf[:, 128:256], identity=identb[:, :])], R=[bobf, bconst], Wr=[bt2_])
                    op(ACT, lambda e, o2=o2, hh=hh, b=b: e.activation(out=oT(hh * 2, b * 128, (b + 1) * 128), in_=tp[:, o2:o2 + 128], func=AF.Copy), R=[bt2_], Wr=[bRA])
                    op(DVE, lambda e, o2=o2, hh=hh, b=b: e.tensor_copy(out=oT(hh * 2 + 1, b * 128, (b + 1) * 128), in_=tp[:, o2 + 128:o2 + 256]), R=[bt2_], Wr=[bRA])
                dma(SP, sts_ds, [lambda e, hh=hh: e.dma_start(out=rst_d[:, hh * 256:(hh + 1) * 256], in_=stv[:, :])], R=[brst], Wr=[bdr[hh]])

        def swa(l, t):
            sc_ = Scr()
            kn, bkn = sc_.get(128, F32); qn, bqn = sc_.get(512, F32); sq, bsq = sc_.get(512, F32)
            r1, br1 = sc_.get(64, F32); r2, br2 = sc_.get(64, F32); r3, br3 = sc_.get(64, F32); r4, br4 = sc_.get(64, F32)
            kr, bkr = sc_.get(128, BF16); qr, bqr = sc_.get(512, BF16); qT, bqT = sc_.get(512, BF16)
            pc, bpc = sc_.get(512, BF16); pp, bpp = sc_.get(512, BF16); ob, bob = sc_.get(512, BF16)
            kTt, _ = sc_.get((NB + 1) * 128, BF16); vt, _ = sc_.get((NB + 1) * 128, BF16)
            h4 = lambda ap: ap.rearrange("p (g c) -> p g c", g=4)
            g16 = lambda ap: ap.rearrange("p (g c) -> p g c", g=4)
            for kh in range(KVH):
                bkT = [fence() for _ in range(NB + 1)]; bv = [fence() for _ in range(NB + 1)]
                if t > 0:
                    op(DVE, lambda e, kh=kh: e.tensor_copy(out=kTt[:, 0:128], in_=kTh[:, kh * 128:(kh + 1) * 128]), R=[bkTh], Wr=[bkT[0]])
                    op(DVE, lambda e, kh=kh: e.tensor_copy(out=vt[:, 0:128], in_=vh[:, kh * 128:(kh + 1) * 128]), R=[bvh], Wr=[bv[0]])
                s1, bs1 = next_chunk()
                for b in range(NB):
                    kTc = kTt[:, (b + 1) * 128:(b + 2) * 128]; vc = vt[:, (b + 1) * 128:(b + 2) * 128]
                    pk, bpk = nacc(); proj_tok(b, s1, bs1, 0, 256, pk, bpk)
                    op(ACT, lambda e, pk=pk: e.activation(out=sq[:, 0:128], in_=pk[:, 0:128], func=AF.Square, accum_out=small[:, 2:3]), R=[bpk], Wr=[bsq, bsm])
                    rstd(small[:, 2:3], small[:, 2:3], 128, bsm)
                    op(DVE, lambda e, pk=pk: e.scalar_tensor_tensor(out=kn, in0=pk[:, 0:128], scalar=small[:, 2:3], in1=kg[:, l * 128:(l + 1) * 128], op0=ALU.mult, op1=ALU.mult), R=[bpk, bsm, bconst], Wr=[bkn])
                    cs = coss[:, b * 64:b * 64 + 16]; sn = sins[:, b * 64:b * 64 + 16]
                    op(DVE, lambda e, cs=cs: e.tensor_tensor(out=r1[:, 0:16], in0=kn[:, 0:16], in1=cs, op=ALU.mult), R=[bkn, btab], Wr=[br1])
                    op(DVE, lambda e, sn=sn: e.tensor_tensor(out=r2[:, 0:16], in0=kn[:, 16:32], in1=sn, op=ALU.mult), R=[bkn, btab], Wr=[br2])
                    op(DVE, lambda e, cs=cs: e.tensor_tensor(out=r3[:, 0:16], in0=kn[:, 16:32], in1=cs, op=ALU.mult), R=[bkn, btab], Wr=[br3])
                    op(DVE, lambda e, sn=sn: e.tensor_tensor(out=r4[:, 0:16], in0=kn[:, 0:16], in1=sn, op=ALU.mult), R=[bkn, btab], Wr=[br4])
                    op(DVE, lambda e: e.tensor_tensor(out=kr[:, 0:16], in0=r1[:, 0:16], in1=r2[:, 0:16], op=ALU.subtract), R=[br1, br2], Wr=[bkr])
                    op(DVE, lambda e: e.tensor_tensor(out=kr[:, 16:32], in0=r3[:, 0:16], in1=r4[:, 0:16], op=ALU.add), R=[br3, br4], Wr=[bkr])
                    op(ACT, lambda e: e.activation(out=kr[:, 32:128], in_=kn[:, 32:128], func=AF.Copy), R=[bkn], Wr=[bkr])
                    o, bt = ntp()
                    group(PE, [lambda e, o=o: e.transpose(out=tp[:, o:o + 128], in_=kr, identity=identb[:, :])], R=[bkr, bconst], Wr=[bt])
                    op(ACT, lambda e, o=o, kTc=kTc: e.activation(out=kTc, in_=tp[:, o:o + 128], func=AF.Copy), R=[bt], Wr=[bkT[b + 1]])
                    op(DVE, lambda e, pk=pk, vc=vc: e.tensor_copy(out=vc, in_=pk[:, 128:256]), R=[bpk], Wr=[bv[b + 1]])
                op(DVE, lambda e, kh=kh: e.tensor_copy(out=kTh[:, kh * 128:(kh + 1) * 128], in_=kTt[:, NB * 128:(NB + 1) * 128]), R=[bkT[NB]], Wr=[bkTh])
                op(DVE, lambda e, kh=kh: e.tensor_copy(out=vh[:, kh * 128:(kh + 1) * 128], in_=vt[:, NB * 128:(NB + 1) * 128]), R=[bv[NB]], Wr=[bvh])
                s2, bs2 = next_chunk()
                for b in range(NB):
                    first = (t == 0 and b == 0)
                    kTc = kTt[:, (b + 1) * 128:(b + 2) * 128]; vc = vt[:, (b + 1) * 128:(b + 2) * 128]
                    kTp = kTt[:, b * 128:(b + 1) * 128]; vp = vt[:, b * 128:(b + 1) * 128]
                    pq, bpq = nacc(); proj_tok(b, s2, bs2, 0, 512, pq, bpq)
                    op(ACT, lambda e, pq=pq: e.activation(out=sq, in_=pq[:, 0:512], func=AF.Square), R=[bpq], Wr=[bsq])
                    op(DVE, lambda e: e.tensor_reduce(out=small[:, 4:8], in_=h4(sq), axis=AX.X, op=ALU.add), R=[bsq], Wr=[bsm])
                    rstd(small[:, 4:8], small[:, 4:8], 128, bsm)
                    for g in range(4):
                        op(DVE, lambda e, g=g, pq=pq: e.scalar_tensor_tensor(out=qn[:, g * 128:(g + 1) * 128], in0=pq[:, g * 128:(g + 1) * 128], scalar=small[:, 4 + g:5 + g], in1=qg[:, l * 128:(l + 1) * 128], op0=ALU.mult, op1=ALU.mult), R=[bpq, bsm, bconst], Wr=[bqn])
                    cs4 = g16(coss[:, b * 64:(b + 1) * 64]); sn4 = g16(sins[:, b * 64:(b + 1) * 64])
                    qn4 = h4(qn); qr4 = h4(qr)
                    op(DVE, lambda e, cs4=cs4: e.tensor_tensor(out=g16(r1), in0=qn4[:, :, 0:16], in1=cs4, op=ALU.mult), R=[bqn, btab], Wr=[br1])
                    op(DVE, lambda e, sn4=sn4: e.tensor_tensor(out=g16(r2), in0=qn4[:, :, 16:32], in1=sn4, op=ALU.mult), R=[bqn, btab], Wr=[br2])
                    op(DVE, lambda e, cs4=cs4: e.tensor_tensor(out=g16(r3), in0=qn4[:, :, 16:32], in1=cs4, op=ALU.mult), R=[bqn, btab], Wr=[br3])
                    op(DVE, lambda e, sn4=sn4: e.tensor_tensor(out=g16(r4), in0=qn4[:, :, 0:16], in1=sn4, op=ALU.mult), R=[bqn, btab], Wr=[br4])
                    op(DVE, lambda e: e.tensor_tensor(out=qr4[:, :, 0:16], in0=g16(r1), in1=g16(r2), op=ALU.subtract), R=[br1, br2], Wr=[bqr])
                    op(DVE, lambda e: e.tensor_tensor(out=qr4[:, :, 16:32], in0=g16(r3), in1=g16(r4), op=ALU.add), R=[br3, br4], Wr=[bqr])
                    op(ACT, lambda e: e.activation(out=qr4[:, :, 32:128], in_=qn4[:, :, 32:128], func=AF.Copy), R=[bqn], Wr=[bqr])
                    o, bt = ntp()
                    group(PE, [(lambda e, o=o, g=g: e.transpose(out=tp[:, o + g * 128:o + (g + 1) * 128], in_=qr[:, g * 128:(g + 1) * 128], identity=identb[:, :])) for g in range(4)], R=[bqr, bconst], Wr=[bt])
                    op(DVE, lambda e, o=o: e.tensor_copy(out=qT, in_=tp[:, o:o + 512]), R=[bt], Wr=[bqT])
                    group(PE, [lambda e, kTc=kTc: e.matmul(scp[0][:, 0:512], lhsT=kTc, rhs=qT, start=True, stop=True)], R=[bkT[b + 1], bqT], Wr=[bsc[0]])
                    op(ACT, lambda e: e.activation(out=pc, in_=scp[0][:, 0:512], func=AF.Exp), R=[bsc[0]], Wr=[bpc])
                    op(DVE, lambda e: e.tensor_tensor(out=pc, in0=pc, in1=mcur[:, :], op=ALU.mult), R=[bpc, bconst], Wr=[bpc])
                    if not first:
                        group(PE, [lambda e, kTp=kTp: e.matmul(scp[1][:, 0:512], lhsT=kTp, rhs=qT, start=True, stop=True)], R=[bkT[b], bqT], Wr=[bsc[1]])
                        op(ACT, lambda e: e.activation(out=pp, in_=scp[1][:, 0:512], func=AF.Exp), R=[bsc[1]], Wr=[bpp])
                        op(DVE, lambda e: e.tensor_tensor(out=pp, in0=pp, in1=mprev[:, :], op=ALU.mult), R=[bpp, bconst], Wr=[bpp])
                    pd, bpd = nacc()
                    mm = []
                    for g in range(4):
                        mm.append(lambda e, g=g, vc=vc, first=first: e.matmul(yo[:, g * 128:(g + 1) * 128], lhsT=pc[:, g * 128:(g + 1) * 128], rhs=vc, start=True, stop=first))
                        if not first:
                            mm.append(lambda e, g=g, vp=vp: e.matmul(yo[:, g * 128:(g + 1) * 128], lhsT=pp[:, g * 128:(g + 1) * 128], rhs=vp, start=False, stop=True))
                        mm.append(lambda e, g=g, pd=pd, first=first: e.matmul(pd[:, g:g + 1], lhsT=pc[:, g * 128:(g + 1) * 128], rhs=ones[:, :], start=True, stop=first))
                        if not first:
                            mm.append(lambda e, g=g, pd=pd: e.matmul(pd[:, g:g + 1], lhsT=pp[:, g * 128:(g + 1) * 128], rhs=ones[:, :], start=False, stop=True))
                    group(PE, mm, R=[bpc, bpp, bv[b + 1], bv[b], bconst], Wr=[byo, bpd])
                    op(DVE, lambda e, pd=pd, kh=kh: e.tensor_tensor(out=small[:, 8:12], in0=pd[:, 0:4], in1=esk[:, l * QH + kh * 4:l * QH + kh * 4 + 4], op=ALU.add), R=[bpd, bconst], Wr=[bsm])
                    op(DVE, lambda e: e.reciprocal(out=small[:, 8:12], in_=small[:, 8:12]), R=[bsm], Wr=[bsm])
                    for g in range(4):
                        if g % 2 == 0:
                            op(ACT, lambda e, g=g: e.activation(out=ob[:, g * 128:(g + 1) * 128], in_=yo[:, g * 128:(g + 1) * 128], func=AF.Copy, scale=small[:, 8 + g:9 + g]), R=[byo, bsm], Wr=[bob])
                        else:
                            op(DVE, lambda e, g=g: e.tensor_scalar(out=ob[:, g * 128:(g + 1) * 128], in0=yo[:, g * 128:(g + 1) * 128], scalar1=small[:, 8 + g:9 + g], scalar2=None, op0=ALU.mult), R=[byo, bsm], Wr=[bob])
                    o, bt = ntp()
                    group(PE, [(lambda e, o=o, g=g: e.transpose(out=tp[:, o + g * 128:o + (g + 1) * 128], in_=ob[:, g * 128:(g + 1) * 128], identity=identb[:, :])) for g in range(4)], R=[bob, bconst], Wr=[bt])
                    for g in range(4):
                        kt = LB + kh * 4 + g
                        if g % 2 == 0:
                            op(ACT, lambda e, o=o, g=g, kt=kt, b=b: e.activation(out=oT(kt, b * 128, (b + 1) * 128), in_=tp[:, o + g * 128:o + (g + 1) * 128], func=AF.Copy), R=[bt], Wr=[bRA])
                        else:
                            op(DVE, lambda e, o=o, g=g, kt=kt, b=b: e.tensor_copy(out=oT(kt, b * 128, (b + 1) * 128), in_=tp[:, o + g * 128:o + (g + 1) * 128]), R=[bt], Wr=[bRA])

        def lru(l, t):
            sc_ = Scr()
            xbuf, bxb = sc_.get(T + 3, F32); xc, bxc = sc_.get(T, F32); rr, brr = sc_.get(T, F32); ii, bii = sc_.get(T, F32)
            aa, baa = sc_.get(T, F32); a2, ba2 = sc_.get(T, F32); hs, bhs = sc_.get(T, F32); gl, bgl = sc_.get(T, F32)
            xcb, bxcb = sc_.get(T, BF16)
            for lb in range(LB):
                s1, bs1 = next_chunk()
                li = l * LB + lb
                px, bpx = nacc(); proj_feat(s1, bs1, 0, px, bpx, lambda k: RB[:, k * T:(k + 1) * T], KT, [bhT])
                py, bpy = nacc(); proj_feat(s1, bs1, 128, py, bpy, lambda k: RB[:, k * T:(k + 1) * T], KT, [bhT])
                if t == 0:
                    op(DVE, lambda e, lb=lb: e.memset(xbuf[:, 0:3], 0.0), Wr=[bxb])
                    op(DVE, lambda e, lb=lb: e.memset(lhs[:, lb:lb + 1], 0.0), Wr=[blh])
                else:
                    op(DVE, lambda e, lb=lb: e.tensor_copy(out=xbuf[:, 0:3], in_=lhalo[:, lb * 3:lb * 3 + 3]), R=[blh], Wr=[bxb])
                op(ACT, lambda e, px=px: e.activation(out=xbuf[:, 3:T + 3], in_=px[:, 0:T], func=AF.Copy), R=[bpx], Wr=[bxb])
                op(DVE, lambda e, lb=lb: e.tensor_copy(out=lhalo[:, lb * 3:lb * 3 + 3], in_=xbuf[:, T:T + 3]), R=[bxb], Wr=[blh])
                op(DVE, lambda e, li=li: e.tensor_scalar(out=xc, in0=xbuf[:, 0:T], scalar1=cw[:, li * 4:li * 4 + 1], scalar2=cb[:, li:li + 1], op0=ALU.mult, op1=ALU.add), R=[bxb, bconst], Wr=[bxc])
                for k in range(1, 4):
                    op(DVE, lambda e, li=li, k=k: e.scalar_tensor_tensor(out=xc, in0=xbuf[:, k:T + k], scalar=cw[:, li * 4 + k:li * 4 + k + 1], in1=xc, op0=ALU.mult, op1=ALU.add), R=[bxb, bconst, bxc], Wr=[bxc])
                op(ACT, lambda e: e.activation(out=xcb, in_=xc, func=AF.Copy), R=[bxc], Wr=[bxcb])
                pr, bpr = nacc()
                group(PE, [lambda e, pr=pr, s1=s1: e.matmul(pr[:, 0:T], lhsT=s1[:, 0, 256:384], rhs=xcb, start=True, stop=True)], R=[bs1, bxcb], Wr=[bpr])
                pi_, bpi = nacc()
                group(PE, [lambda e, pi_=pi_, s1=s1: e.matmul(pi_[:, 0:T], lhsT=s1[:, 1, 256:384], rhs=xcb, start=True, stop=True)], R=[bs1, bxcb], Wr=[bpi])
                op(ACT, lambda e, pr=pr, li=li: e.activation(out=rr, in_=pr[:, 0:T], func=AF.Sigmoid, bias=bas[:, li:li + 1]), R=[bpr, bconst], Wr=[brr])
                op(ACT, lambda e, pi_=pi_, li=li: e.activation(out=ii, in_=pi_[:, 0:T], func=AF.Sigmoid, bias=bxs[:, li:li + 1]), R=[bpi, bconst], Wr=[bii])
                op(ACT, lambda e, li=li: e.activation(out=aa, in_=rr, func=AF.Exp, scale=cc[:, li:li + 1]), R=[brr, bconst], Wr=[baa])
                op(ACT, lambda e, li=li: e.activation(out=a2, in_=rr, func=AF.Exp, scale=cc2[:, li:li + 1]), R=[brr, bconst], Wr=[ba2])
                op(ACT, lambda e: e.activation(out=a2, in_=a2, func=AF.Sqrt, scale=-1.0, bias=1.0), R=[ba2], Wr=[ba2])
                op(DVE, lambda e: e.tensor_tensor(out=ii, in0=ii, in1=xc, op=ALU.mult), R=[bii, bxc], Wr=[bii])
                op(DVE, lambda e: e.tensor_tensor(out=ii, in0=ii, in1=a2, op=ALU.mult), R=[bii, ba2], Wr=[bii])
                op(DVE, lambda e, lb=lb: e.tensor_tensor_scan(out=hs, data0=aa, data1=ii, initial=lhs[:, lb:lb + 1], op0=ALU.mult, op1=ALU.add), R=[baa, bii, blh], Wr=[bhs])
                op(DVE, lambda e, lb=lb: e.tensor_copy(out=lhs[:, lb:lb + 1], in_=hs[:, T - 1:T]), R=[bhs], Wr=[blh])
                op(ACT, lambda e, py=py: e.activation(out=gl, in_=py[:, 0:T], func=AF.Gelu), R=[bpy], Wr=[bgl])
                op(DVE, lambda e, lb=lb: e.tensor_tensor(out=oT(2 * LB + lb, 0, T), in0=hs, in1=gl, op=ALU.mult), R=[bhs, bgl], Wr=[bRA])

        def merge(l, t):
            base = OTN // 2
            accs = [RA[:, base + i * 512:base + i * 512 + T] for i in range(4)]
            sgs = [RA[:, base + 2048 + i * 512:base + 2048 + i * 512 + T] for i in range(4)]
            tmp = rtmp[:, 0:T]
            fence(bRC)
            bac = [fence() for _ in range(4)]; bsg = [fence() for _ in range(4)]; btm = brt[0]
            for dg in range(D // 512):
                for k in range(3):
                    s1, bs1 = next_chunk()
                    for j in range(4):
                        pg, bpg = nacc(); proj_feat(s1, bs1, j * 128, pg, bpg, lambda kk: RB[:, kk * T:(kk + 1) * T], KT, [bhT])
                        op(ACT, lambda e, pg=pg, j=j: e.activation(out=sgs[j], in_=pg[:, 0:T], func=AF.Sigmoid), R=[bpg], Wr=[bsg[j]])
                    s2, bs2 = next_chunk()
                    for j in range(4):
                        dt_ = dg * 4 + j
                        pp_, bpp_ = nacc(); proj_feat(s2, bs2, j * 128, pp_, bpp_, lambda kk, k=k: oT(k * LB + kk, 0, T), LB, [bRA])
                        if k == 0:
                            op(DVE, lambda e, j=j, pp_=pp_: e.tensor_tensor(out=accs[j], in0=sgs[j], in1=pp_[:, 0:T], op=ALU.mult), R=[bsg[j], bpp_], Wr=[bac[j]])
                        elif k == 1:
                            op(DVE, lambda e, j=j, pp_=pp_: e.tensor_tensor(out=tmp, in0=sgs[j], in1=pp_[:, 0:T], op=ALU.mult), R=[bsg[j], bpp_], Wr=[btm])
                            op(DVE, lambda e, j=j: e.tensor_tensor(out=accs[j], in0=accs[j], in1=tmp, op=ALU.add), R=[btm, bac[j]], Wr=[bac[j]])
                        else:
                            op(DVE, lambda e, j=j, pp_=pp_: e.tensor_tensor(out=tmp, in0=sgs[j], in1=pp_[:, 0:T], op=ALU.mult), R=[bsg[j], bpp_], Wr=[btm])
                            op(DVE, lambda e, j=j, dt_=dt_: e.tensor_tensor(out=RC[:, dt_ * T:(dt_ + 1) * T], in0=accs[j], in1=tmp, op=ALU.add), R=[btm, bac[j]], Wr=[bRC])

        def outproj(l, t, xin):
            fence(bRA)
            load_x(xin, t)
            for dg in range(D // 512):
                s1, bs1 = next_chunk()
                for b in range(NB):
                    pa, bpa = nacc()
                    group(PE, [(lambda e, k=k, pa=pa, b=b, s1=s1: e.matmul(pa[:, 0:512], lhsT=RC[:, k * T + b * 128:k * T + (b + 1) * 128], rhs=s1[:, k, 0:512], start=(k == 0), stop=(k == KT - 1))) for k in range(KT)],
                          R=[bRC, bs1], Wr=[bpa])
                    xs = RA[:, b * D + dg * 512:b * D + (dg + 1) * 512]
                    op(DVE, lambda e, xs=xs, pa=pa: e.tensor_tensor(out=xs, in0=xs, in1=pa[:, 0:512], op=ALU.add), R=[bpa, bRA], Wr=[bRA])

        def mlp(l, t):
            fence(bRC)
            baT = [fence(), fence()]
            for fc in range(FT // 16):
                ab = fc % 2
                aoff = ab * 16 * T
                for q in range(4):
                    s1, bs1 = next_chunk()
                    for j in range(4):
                        fl = q * 4 + j
                        pu, bpu = nacc(); proj_feat(s1, bs1, j * 128, pu, bpu, lambda kk: RB[:, kk * T:(kk + 1) * T], KT, [bhT])
                        rt = rtmp[:, 0:T]
                        op(ACT, lambda e, pu=pu, rt=rt: e.activation(out=rt, in_=pu[:, 0:T], func=AF.Relu), R=[bpu], Wr=[brt[0]])
                        op(DVE, lambda e, rt=rt, fl=fl, aoff=aoff: e.tensor_tensor(out=RC[:, aoff + fl * T:aoff + (fl + 1) * T], in0=rt, in1=rt, op=ALU.mult), R=[brt[0]], Wr=[baT[ab]])
                for dg in range(D // 512):
                    s1, bs1 = next_chunk()
                    for b in range(NB):
                        pa, bpa = nacc()
                        group(PE, [(lambda e, k=k, pa=pa, b=b, aoff=aoff, s1=s1: e.matmul(pa[:, 0:512], lhsT=RC[:, aoff + k * T + b * 128:aoff + k * T + (b + 1) * 128], rhs=s1[:, k, 0:512], start=(k == 0), stop=(k == 15))) for k in range(16)],
                              R=[baT[ab], bs1], Wr=[bpa])
                        xs = RA[:, b * D + dg * 512:b * D + (dg + 1) * 512]
                        op(DVE, lambda e, xs=xs, pa=pa: e.tensor_tensor(out=xs, in0=xs, in1=pa[:, 0:512], op=ALU.add), R=[bpa, bRA], Wr=[bRA])
            return baT

        def store_x(xout, t):
            fns = []
            for b in range(NB):
                r0 = t * T + b * 128
                fns.append(lambda e, b=b, r0=r0: e.dma_start(out=xout[r0:r0 + 128, :], in_=RA[:, b * D:(b + 1) * D]))
            return dma(SP, xs_ds, fns, R=[bRA], Wr=[bxd])

        bxd = Buf()
        for l in range(L):
            xin = x_d if l == 0 else xs_d[l - 1]
            xout = y_d if l == L - 1 else xs_d[l]
            for t in range(NT):
                SP.wait(bxd.rdeps())
                load_x(xin, t)
                if STOP >= 0: tables(t)
                if STOP >= 1: norm_phase(g1, l * KT)
                dbg("hT", RB[:, :], KT * T, BF16, [bhT])
                if STOP >= 2: retention(l, t)
                if STOP >= 3: swa(l, t)
                if STOP >= 4: lru(l, t)
                dbg("oT", RAb[:, 0:OTN], OTN, BF16, [bRA])
                if STOP >= 5: merge(l, t)
                dbg("mT", RC[:, 0:KT * T], KT * T, BF16, [bRC])
                if STOP >= 6: outproj(l, t, xin)
                dbg("x1", RA[:, 0:NB * D], NB * D, F32, [bRA])
                if STOP >= 7: norm_phase(g2, l * KT)
                if STOP >= 8: baT = mlp(l, t)
                last = store_x(xout, t)
        SP.wait([last])

        with nc.Block() as block:
            @block.tensor
            def _(e): PE.replay(e)
            @block.scalar
            def _(e): ACT.replay(e)
            @block.vector
            def _(e): DVE.replay(e)
            @block.gpsimd
            def _(e): PL.replay(e)
            @block.sync
            def _(e): SP.replay(e)
    return nc


def module_constants(cfg):
    RH = cfg.RH
    h = np.arange(RH, dtype=np.float64)
    log_g = np.log1p(-np.exp2(-5.0 - h))
    i = np.arange(128, dtype=np.float64)
    diff = i[None, :] - i[:, None]
    dtab = np.where(diff[None] >= 0, np.exp(log_g[:, None, None] * np.maximum(diff, 0)[None]), 0.0) * 128.0 ** -0.5
    dtab = np.ascontiguousarray(dtab.transpose(1, 0, 2).reshape(128, RH * 128)).astype(np.float32)
    xi = np.exp(log_g[None, :] * (i[:, None] + 1.0)).astype(np.float32)
    zeta = (np.exp(log_g[None, :] * (127.0 - i[:, None])) * 128.0 ** -0.5).astype(np.float32)
    gch = np.broadcast_to(np.exp(log_g * 128.0)[None, :], (128, RH)).astype(np.float32)
    s = np.arange(128)[:, None]; q = np.arange(128)[None, :]
    mcur = np.tile((s <= q).astype(np.float32), (1, 4)); mprev = np.tile((s > q).astype(np.float32), (1, 4))
    fr = ((10000.0 ** (-np.arange(64, dtype=np.float32) / 64)).astype(np.float32) / np.float32(2 * np.pi)).astype(np.float32)
    fs = ((500000.0 ** (-np.arange(16, dtype=np.float32) / 16)).astype(np.float32) / np.float32(2 * np.pi)).astype(np.float32)
    fr_ret = np.broadcast_to(np.tile(fr, 2)[None, :], (128, 128)).astype(np.float32)
    fr_swa = np.broadcast_to(np.tile(fs, 4)[None, :], (128, 64)).astype(np.float32)
    bf = ml_dtypes.bfloat16
    return dict(ident=np.eye(128, dtype=np.float32), dtab=dtab.astype(bf), xi=xi, zeta=zeta, gch=np.ascontiguousarray(gch),
                mcur=np.ascontiguousarray(mcur).astype(bf), mprev=np.ascontiguousarray(mprev).astype(bf),
                fr_ret=np.ascontiguousarray(fr_ret), fr_swa=np.ascontiguousarray(fr_swa))


def host_layout(cfg, inp, bidx):
    L, KT, LB, QH, W, D = cfg.L, cfg.KT, cfg.LB, cfg.QH, cfg.W, cfg.D
    c = np.ascontiguousarray
    col = lambda v, n: c(np.asarray(v, np.float32).reshape(L, n, 128).transpose(2, 0, 1).reshape(128, L * n))
    bc = lambda v: c(np.broadcast_to(np.asarray(v, np.float32).reshape(1, -1), (128, v.size)))
    m = dict(
        x=c(inp["x"][bidx]), pos=c(np.asarray(inp["positions"][bidx], np.int32).reshape(-1, 128).T),
        w_in=inp["w_in"], w_branch=np.asarray(inp["w_branch"]).reshape(L, 3 * W, D), w_out=inp["w_out"],
        w_mlp_in=inp["w_mlp_in"], w_mlp_out=inp["w_mlp_out"],
        g1=col(inp["norm_mix"], KT), g2=col(inp["norm_mlp"], KT),
        qg=bc(inp["swa_q_gain"]), kg=bc(inp["swa_k_gain"]), sinkb=bc(inp["swa_sinks"]),
        cw=c(np.asarray(inp["conv_w"], np.float32).reshape(L, 4, LB, 128).transpose(3, 0, 2, 1).reshape(128, L * LB * 4)),
        cb=col(inp["conv_b"], LB), ba=col(inp["lru_ba"], LB), bx=col(inp["lru_bx"], LB), lam=col(inp["lru_lambda"], LB),
        wa=c(np.asarray(inp["lru_wa"], np.float32).transpose(0, 2, 1, 3).reshape(L, 128, LB * 128)),
        wx=c(np.asarray(inp["lru_wx"], np.float32).transpose(0, 2, 1, 3).reshape(L, 128, LB * 128)),
    )
    m.update(module_constants(cfg))
    return m


def run(cfg, inp, nb):
    nc = build(cfg)
    maps = [host_layout(cfg, inp, b) for b in range(nb)]
    res = run_bass_kernel_spmd(nc, maps, core_ids=list(range(nb)), **({'trace': True} if os.environ.get('KTRACE') else {}))
    if os.environ.get('KTRACE'):
        print('EXEC_NS', res.exec_time_ns)
    if os.environ.get('KDBG'):
        return np.stack([np.asarray(r["y"]) for r in res.results], 0), np.asarray(res.results[0]["dbg"])
    return np.stack([np.asarray(r["y"]) for r in res.results], 0)


PER_LAYER_KEYS = ("norm_mix", "w_in", "swa_q_gain", "swa_k_gain", "swa_sinks", "conv_w", "conv_b", "lru_wa", "lru_ba",
                  "lru_wx", "lru_bx", "lru_lambda", "w_branch", "w_out", "norm_mlp", "w_mlp_in", "w_mlp_out")


def kernel(**inputs):
    inp = {k: np.asarray(v) for k, v in inputs.items()}
    B, S, D = inp["x"].shape
    L = inp["w_in"].shape[0]
    if os.environ.get("KFUSED", "0") == "1":
        cfg = Cfg(D=D, S=S, L=L, T=512)
        return run(cfg, inp, B).astype(np.float32)
    cfg = Cfg(D=D, S=S, L=1, T=512)
    nc = build(cfg)
    x = inp["x"]
    for l in range(L):
        li = dict(inp)
        li["x"] = x
        for k in PER_LAYER_KEYS:
            li[k] = inp[k][l:l + 1]
        maps = [host_layout(cfg, li, b) for b in range(B)]
        res = run_bass_kernel_spmd(nc, maps, core_ids=list(range(B)))
        x = np.stack([np.asarray(r["y"]) for r in res.results], 0)
    return x.astype(np.float32)
```
